# Optimizing a Trainium2 kernel written in Bass

```python
import jax, jax.numpy as jnp
from jax import lax
import numpy as np

D_MODEL = 2048
BATCH = 4
SEQ = 2048
DEPTH = 2

POOL_WIDTH = 1024
POOL_GROUPS = 4
POOL_WINDOWS = (2, 4, 8, 16)
POOL_GROUP_DIM = POOL_WIDTH // POOL_GROUPS
CONF_WIDTH = 1024
CONF_KERNEL = 31
SCONV_WIDTH = 1024
SCONV_KERNEL = 3
N_BRANCH = 3
OFF_POOL = POOL_WIDTH
OFF_CONF = OFF_POOL + 2 * CONF_WIDTH
OFF_SCONV = OFF_CONF + 3 * SCONV_WIDTH
D_IN = OFF_SCONV + N_BRANCH * D_MODEL
D_FF = 5504
FFN_KERNEL = 3
EPS = 1e-6

kernel_name = 'hybrid_pool_conformer_shortconv_block'


def rms_norm(x, g):
    xf = x.astype(jnp.float32)
    y = xf * lax.rsqrt(jnp.mean(xf * xf, axis=-1, keepdims=True) + EPS)
    return (y * g.astype(jnp.float32)).astype(x.dtype)


def layer_norm(x, g, b):
    xf = x.astype(jnp.float32)
    mu = jnp.mean(xf, axis=-1, keepdims=True)
    xc = xf - mu
    var = jnp.mean(xc * xc, axis=-1, keepdims=True)
    y = xc * lax.rsqrt(var + EPS) * g.astype(jnp.float32) + b.astype(jnp.float32)
    return y.astype(x.dtype)


def causal_dwconv(x, w):
    k, c = w.shape
    return lax.conv_general_dilated(
        x, w[:, None, :].astype(x.dtype), window_strides=(1,), padding=[(k - 1, 0)],
        dimension_numbers=('NWC', 'WIO', 'NWC'), feature_group_count=c)


def multiscale_pool(u, w_grp, scale):
    b, s, _ = u.shape
    ug = u.reshape(b, s, POOL_GROUPS, POOL_GROUP_DIM).astype(jnp.float32)
    cs0 = jnp.pad(jnp.cumsum(ug, axis=1), ((0, 0), (1, 0), (0, 0), (0, 0)))
    pos = jnp.arange(1, s + 1, dtype=jnp.float32)
    outs = []
    for g, w in enumerate(POOL_WINDOWS):
        cg = cs0[:, :, g]
        lagged = jnp.pad(cg[:, :s + 1 - w], ((0, 0), (w, 0), (0, 0)))
        window_sum = cg[:, 1:] - lagged[:, 1:]
        count = jnp.minimum(pos, float(w))[None, :, None]
        outs.append(window_sum / count - ug[:, :, g])
    pooled = jnp.stack(outs, axis=2).astype(u.dtype)
    mixed = jnp.einsum('bsgc,gcd->bsgd', pooled, w_grp)
    return mixed.reshape(b, s, POOL_WIDTH) * scale


def setup_inputs(seed: int = 0) -> dict:
    key = jax.random.key(seed)
    ks = jax.random.split(key, 24)
    L, D = DEPTH, D_MODEL
    nrm = lambda k, shape, fan: jax.random.normal(k, shape, jnp.float32) * (fan ** -0.5)
    gain = lambda k, shape: 1.0 + 0.05 * jax.random.normal(k, shape, jnp.float32)
    small = lambda k, shape: 0.02 * jax.random.normal(k, shape, jnp.float32)
    return {
        'x': jax.random.normal(ks[0], (BATCH, SEQ, D), jnp.float32),
        'norm1_g': gain(ks[1], (L, D)),
        'w_in': nrm(ks[2], (L, D, D_IN), D),
        'gate_b': small(ks[3], (L, N_BRANCH * D)),
        'pool_w': nrm(ks[4], (L, POOL_GROUPS, POOL_GROUP_DIM, POOL_GROUP_DIM), POOL_GROUP_DIM),
        'pool_scale': gain(ks[5], (L, POOL_WIDTH)),
        'pool_proj': nrm(ks[6], (L, POOL_WIDTH, D), POOL_WIDTH),
        'conf_conv_w': nrm(ks[7], (L, CONF_KERNEL, CONF_WIDTH), CONF_KERNEL),
        'conf_conv_b': small(ks[8], (L, CONF_WIDTH)),
        'conf_ln_g': gain(ks[9], (L, CONF_WIDTH)),
        'conf_ln_b': small(ks[10], (L, CONF_WIDTH)),
        'conf_proj': nrm(ks[11], (L, CONF_WIDTH, D), CONF_WIDTH),
        'sconv_w': nrm(ks[12], (L, SCONV_KERNEL, SCONV_WIDTH), SCONV_KERNEL),
        'sconv_proj': nrm(ks[13], (L, SCONV_WIDTH, D), SCONV_WIDTH),
        'w_o': nrm(ks[14], (L, D, D), D),
        'norm2_g': gain(ks[15], (L, D)),
        'ffn_up': nrm(ks[16], (L, D, 2 * D_FF), D),
        'ffn_conv_w': nrm(ks[17], (L, FFN_KERNEL, 2 * D_FF), FFN_KERNEL),
        'ffn_down': nrm(ks[18], (L, D_FF, D), D_FF),
        'final_g': gain(ks[19], (D,)),
    }


def reference(x, norm1_g, w_in, gate_b, pool_w, pool_scale, pool_proj, conf_conv_w, conf_conv_b,
              conf_ln_g, conf_ln_b, conf_proj, sconv_w, sconv_proj, w_o, norm2_g, ffn_up, ffn_conv_w,
              ffn_down, final_g):
    for l in range(DEPTH):
        h = rms_norm(x, norm1_g[l])
        z = h @ w_in[l]
        u_pool, u_conf, u_sc, gate_logits = jnp.split(z, [OFF_POOL, OFF_CONF, OFF_SCONV], axis=-1)
        br_a = multiscale_pool(u_pool, pool_w[l], pool_scale[l]) @ pool_proj[l]
        val, gte = jnp.split(u_conf, 2, axis=-1)
        v = val * jax.nn.sigmoid(gte)
        v = causal_dwconv(v, conf_conv_w[l]) + conf_conv_b[l]
        v = jax.nn.silu(layer_norm(v, conf_ln_g[l], conf_ln_b[l]))
        br_b = v @ conf_proj[l]
        bg, cg, hs = jnp.split(u_sc, 3, axis=-1)
        br_c = (bg * causal_dwconv(cg * hs, sconv_w[l])) @ sconv_proj[l]
        ga, gb, gc = jnp.split(jax.nn.sigmoid(gate_logits + gate_b[l]), 3, axis=-1)
        x = x + (ga * br_a + gb * br_b + gc * br_c) @ w_o[l]
        h = rms_norm(x, norm2_g[l])
        up = causal_dwconv(h @ ffn_up[l], ffn_conv_w[l])
        gt, vl = jnp.split(up, 2, axis=-1)
        x = x + (jax.nn.silu(gt) * vl) @ ffn_down[l]
    return rms_norm(x, final_g)
```

```python
import contextlib
import numpy as np
import concourse.bass as bass
import concourse.mybir as mybir
from concourse.bass_utils import run_bass_kernel_spmd

F32 = mybir.dt.float32
BF16 = mybir.dt.bfloat16
F32R = mybir.dt.float32r
ALU = mybir.AluOpType
AF = mybir.ActivationFunctionType

D = 2048
NDC = 16
TT = 1056
TILES = [(0, 272), (272, 512), (784, 272)]
HALO = 64
SEQ = 2048
EPS = 1e-6
NSLOT = 6
PADL = 32
DFF = 5504
NFC = 43
FGROUPS = [(0, 11), (11, 11), (22, 11), (33, 10)]

PV_N1G = 0
PV_GB = 16
PV_PSC = 64
PV_CCW = 72
PV_CCB = 320
PV_LNG = 328
PV_LNB = 336
PV_SCW = 344
PV_N2G = 368
PV_FCW = 384
PV_L = 642
PV_FG = 2 * PV_L
PV_FAC = PV_FG + 16
PV_ID = PV_FAC + 64
NPV = PV_ID + 128


class Stream:
    def __init__(self):
        self.items = []
        self.waited = {}


class Plan:
    def __init__(self):
        self.streams = {k: Stream() for k in ("pe", "act", "dve", "pool", "sp")}
        self.count = {}
        self.lastw = {}
        self.readers = {}

    def _need(self, reads, writes):
        need = {}

        def add(tok):
            if tok is None:
                return
            s, v = tok
            if need.get(s, 0) < v:
                need[s] = v

        for k in reads:
            add(self.lastw.get(k))
        for k in writes:
            add(self.lastw.get(k))
            for s, v in self.readers.get(k, {}).items():
                add((s, v))
        return need

    def op(self, eng, fn, reads=(), writes=(), sem=None, inc=1):
        st = self.streams[eng]
        if eng in ("act", "dve"):
            extra = [("psr", k[1]) for k in reads if isinstance(k, tuple) and k[0] == "ps"]
            if extra:
                writes = list(writes) + extra
        need = self._need(reads, writes)
        for s, v in need.items():
            if st.waited.get(s, 0) < v:
                st.waited[s] = v
                st.items.append(("wait", s, v))
        if sem is None:
            sem = eng
        self.count[sem] = self.count.get(sem, 0) + inc
        tok = (sem, self.count[sem])
        st.items.append(("op", fn, sem, inc))
        for k in reads:
            r = self.readers.setdefault(k, {})
            if r.get(sem, 0) < tok[1]:
                r[sem] = tok[1]
        for k in writes:
            self.lastw[k] = tok
            self.readers[k] = {}
        return tok

    def wait_all(self, eng, keys):
        st = self.streams[eng]
        need = self._need(keys, keys)
        for s, v in need.items():
            if st.waited.get(s, 0) < v:
                st.waited[s] = v
                st.items.append(("wait", s, v))

    def alias(self, old, new):
        acc = {}
        for k in old:
            t = self.lastw.get(k)
            if t is not None and acc.get(t[0], 0) < t[1]:
                acc[t[0]] = t[1]
            for s, v in self.readers.get(k, {}).items():
                if acc.get(s, 0) < v:
                    acc[s] = v
        for k in new:
            r = dict(acc)
            t = self.lastw.get(k)
            if t is not None and r.get(t[0], 0) < t[1]:
                r[t[0]] = t[1]
            for s, v in self.readers.get(k, {}).items():
                if r.get(s, 0) < v:
                    r[s] = v
            self.lastw[k] = None
            self.readers[k] = r

    def emit(self, eng, e, sems):
        for it in self.streams[eng].items:
            if it[0] == "wait":
                e.wait_ge(sems[it[1]], it[2])
            else:
                ins = it[1](e)
                ins.then_inc(sems[it[2]], it[3])


def build_program(layers, final):
    nc = bass.Bass("TRN2", target_bir_lowering=False)
    xin = nc.dram_tensor("xT", [128, NDC, TT], F32, kind="ExternalInput").ap()
    pvd = nc.dram_tensor("pv", [128, NPV], F32, kind="ExternalInput").ap()
    w_in = nc.dram_tensor("w_in", [2, D, 12288], F32, kind="ExternalInput").ap()
    pool_w = nc.dram_tensor("pool_w", [2, 4, 256, 256], F32, kind="ExternalInput").ap()
    pool_proj = nc.dram_tensor("pool_proj", [2, 1024, D], F32, kind="ExternalInput").ap()
    conf_proj = nc.dram_tensor("conf_proj", [2, 1024, D], F32, kind="ExternalInput").ap()
    sconv_proj = nc.dram_tensor("sconv_proj", [2, 1024, D], F32, kind="ExternalInput").ap()
    w_o = nc.dram_tensor("w_o", [2, D, D], F32, kind="ExternalInput").ap()
    ffn_up = nc.dram_tensor("ffn_up", [2, D, 2 * DFF], F32, kind="ExternalInput").ap()
    ffn_down = nc.dram_tensor("ffn_down", [2, DFF, D], F32, kind="ExternalInput").ap()
    yout = nc.dram_tensor("yT", [128, NDC, TT], F32, kind="ExternalOutput").ap()

    with contextlib.ExitStack() as ctx:
        def sb(name, shape, dt):
            return ctx.enter_context(nc.sbuf_tensor(name, shape, dt))

        x = sb("x", [128, NDC, TT], F32)
        h = sb("h", [128, NDC, TT], BF16)
        big = sb("big", [128, 12 * TT], F32)
        wsl = [sb(f"ws{i}", [128, 2048], BF16) for i in range(NSLOT)]
        pv = sb("pvs", [128, NPV], F32)
        T = [sb(f"T{i}", [128, TT], F32) for i in range(5)]
        pads = [sb(f"pad{i}", [128, PADL + TT], BF16) for i in range(2)]
        ones_rms = sb("ones_rms", [128, 128], F32)
        ones_ln = sb("ones_ln", [128, 128], F32)
        ones_rms_b = sb("ones_rms_b", [128, 128], BF16)
        ones_ln_b = sb("ones_ln_b", [128, 128], BF16)
        poolI = sb("poolI", [128, 8, 128], BF16)
        small = sb("small", [128, 16], F32)
        small_u = sb("small_u", [128, 16], F32)
        epsc = sb("epsc", [128, 8], F32)
        ps = ctx.enter_context(nc.psum_tensor("ps", [128, 4096], F32))

        sem_names = (["pe", "act", "dve", "pool", "init", "out"] + [f"w{i}" for i in range(NSLOT)]
                     + [f"x{q}" for q in range(8)])
        sems = {n: ctx.enter_context(nc.semaphore(n)) for n in sem_names}
        block = ctx.enter_context(nc.Block())

        P = Plan()
        SBASE = [240, 1776]
        st = {"slot": 0, "slab": 0, "t01": 0, "t23": 0, "pad": 0}

        ident = pv[:, PV_ID:PV_ID + 128]
        vconv = [big[:, c * TT:(c + 1) * TT] for c in range(8)]
        mflat = big[:, 0:8 * TT].bitcast(BF16)
        m = [mflat[:, j * TT:(j + 1) * TT] for j in range(16)]
        bbflat = big[:, 8 * TT:12 * TT].bitcast(BF16)
        bb = [bbflat[:, c * TT:(c + 1) * TT] for c in range(8)]
        actflat = [big[:, 0:5808].bitcast(BF16), big[:, 5808:11616].bitcast(BF16)]

        def actc(b, i):
            return actflat[b][:, i * TT:(i + 1) * TT]

        def pcol(c):
            return pv[:, c:c + 1]

        def next_slot():
            s = st["slot"]
            st["slot"] = 1 - s
            return s

        def psl(s):
            return ps[:, SBASE[s]:SBASE[s] + TT]

        def nxt(name, base):
            i = st[name]
            st[name] = 1 - i
            return base + i

        def load_slab(src, kc):
            s = st["slab"] % NSLOT
            st["slab"] += 1
            dst = wsl[s][:, 0:kc * 128].rearrange("p (k m) -> p k m", m=128)
            P.op("pool", lambda e: e.dma_start(out=dst, in_=src),
                 reads=(), writes=(("w", s),), sem=f"w{s}", inc=16)
            return s

        def wview(s, k):
            return wsl[s][:, k * 128:(k + 1) * 128]

        def slab_src(W2d, r0, kc, c0):
            return W2d[r0:r0 + kc * 128, c0:c0 + 128].rearrange("(k p) m -> p k m", p=128)

        def diag_slab(cols):
            s = st["slab"] % NSLOT
            st["slab"] += 1
            for i, col in enumerate(cols):
                P.op("act", lambda e, i=i, col=col: e.activation(
                    out=wview(s, i), in_=ident, func=AF.Identity, scale=pcol(col)),
                    reads=("pv",), writes=(("w", s),))
            return s

        def mm(parts, slot, reads, first=True, last=True, fp32r=False):
            n = len(parts)

            def fn(e):
                ins = None
                for i, (lt, rf) in enumerate(parts):
                    for (t0, tn) in TILES:
                        o = ps[:, SBASE[slot] + t0:SBASE[slot] + t0 + tn]
                        ins = e.matmul(out=o, lhsT=lt, rhs=rf(t0, tn),
                                       start=(first and i == 0), stop=(last and i == n - 1))
                return ins
            return P.op("pe", fn, reads=reads, writes=(("ps", slot),))

        def rhs_chunks(buf3, c):
            return lambda t0, tn: buf3[:, c, t0:t0 + tn]

        def rhs_list(lst, c):
            return lambda t0, tn: lst[c][:, t0:t0 + tn]

        def rhs_shift(padbuf, off):
            return lambda t0, tn: padbuf[:, off + t0:off + t0 + tn]

        def proj_h(W2d, col0):
            s = load_slab(slab_src(W2d, 0, 16, col0), 16)
            slot = next_slot()
            if st.get("perk"):
                st["perk"] = False
                for k in range(16):
                    mm([(wview(s, k), rhs_chunks(h, k))], slot, reads=[("w", s), ("h", k)],
                       first=(k == 0), last=(k == 15))
                return slot
            mm([(wview(s, k), rhs_chunks(h, k)) for k in range(16)], slot,
               reads=[("w", s)] + [("h", k) for k in range(16)])
            return slot

        def act(out, in_, func, reads, writes, bias=None, scale=None):
            kw = {}
            if bias is not None:
                kw["bias"] = bias
            if scale is not None:
                kw["scale"] = scale
            return P.op("act", lambda e: e.activation(out=out, in_=in_, func=func, **kw),
                        reads=reads, writes=writes)

        def tt(out, a, b, op, reads, writes):
            return P.op("dve", lambda e: e.tensor_tensor(out=out, in0=a, in1=b, op=op),
                        reads=reads, writes=writes)

        def ts(out, a, s1, s2, op0, op1, reads, writes):
            if op1 is None:
                return P.op("dve", lambda e: e.tensor_scalar(out=out, in0=a, scalar1=s1, scalar2=None, op0=op0),
                            reads=reads, writes=writes)
            return P.op("dve", lambda e: e.tensor_scalar(out=out, in0=a, scalar1=s1, scalar2=s2, op0=op0, op1=op1),
                        reads=reads, writes=writes)

        def stt(out, a, sc, b, op0, op1, reads, writes):
            return P.op("dve", lambda e: e.scalar_tensor_tensor(out=out, in0=a, scalar=sc, in1=b, op0=op0, op1=op1),
                        reads=reads, writes=writes)

        P.op("sp", lambda e: e.dma_start(out=pv[:, :], in_=pvd[:, :]), writes=("pv",), sem="init", inc=16)
        for q in range(8):
            P.op("sp", lambda e, q=q: e.dma_start(out=x[:, 2 * q:2 * q + 2, :], in_=xin[:, 2 * q:2 * q + 2, :]),
                 writes=[("x", j) for j in range(2 * q, 2 * q + 2)], sem=f"x{q}", inc=16)

        P.op("dve", lambda e: e.memset(ones_rms[:, :], 1.0 / D), writes=("ones_rms",))
        P.op("dve", lambda e: e.memset(epsc[:, :], EPS), writes=("epsc",))
        P.op("dve", lambda e: e.memset(ones_rms_b[:, :], 1.0 / D), writes=("ones_rms",))
        P.op("dve", lambda e: e.memset(ones_ln_b[:, :], 1.0 / 1024.0), writes=("ones_ln",))
        P.op("dve", lambda e: e.memset(ones_ln[:, :], 1.0 / 1024.0), writes=("ones_ln",))
        for i in range(2):
            P.op("dve", lambda e, i=i: e.memset(pads[i][:, 0:PADL], 0.0), writes=(("padz", i),))
        for g in range(4):
            w = 2 << g
            ts(poolI[:, 2 * g, :], ident, 1.0 / w - 1.0, None, ALU.mult, None, ["pv"], ["poolI"])
            ts(poolI[:, 2 * g + 1, :], ident, 1.0 / w, None, ALU.mult, None, ["pv"], ["poolI"])

        def sq_acc(c):
            ti = nxt("t01", 0)
            act(T[ti][:, :], x[:, c, :], AF.Square, [("x", c)], [("T", ti)])
            if c == 0:
                P.op("dve", lambda e, ti=ti: e.tensor_copy(out=T[4][:, :], in_=T[ti][:, :]),
                     reads=[("T", ti)], writes=[("T", 4)])
            else:
                tt(T[4][:, :], T[4][:, :], T[ti][:, :], ALU.add, [("T", 4), ("T", ti)], [("T", 4)])

        def rms_stats(rstd_T, pre=False):
            if not pre:
                for c in range(16):
                    sq_acc(c)
            slot = next_slot()
            mm([(ones_rms[:, :], lambda t0, tn: T[4][:, t0:t0 + tn])], slot, reads=[("T", 4), "ones_rms"])
            act(T[rstd_T][:, :], psl(slot), AF.Sqrt, [("ps", slot), "epsc"], [("T", rstd_T)], bias=epsc[:, 0:1])
            P.op("dve", lambda e: e.reciprocal(out=T[rstd_T][:, :], in_=T[rstd_T][:, :]),
                 reads=[("T", rstd_T)], writes=[("T", rstd_T)])

        def rmsnorm_h(gcol, pre=False):
            rms_stats(4, pre)
            for c in range(16):
                stt(h[:, c, :], x[:, c, :], pcol(gcol + c), T[4][:, :], ALU.mult, ALU.mult,
                    [("x", c), ("T", 4), "pv"], [("h", c)])
            st["perk"] = True

        def gate_branch(br, projW, first, L, l, pre=None):
            LA = 2 if first else 1
            tbufs = [0, 1, 2] if first else [0, 1]

            def logits(j):
                slot = proj_h(w_in[l], 6144 + br * 2048 + j * 128)
                ti = tbufs[j % len(tbufs)]
                act(T[ti][:, :], psl(slot), AF.Sigmoid, [("ps", slot), "pv"], [("T", ti)],
                    bias=pcol(L + PV_GB + br * 16 + j))
                return ti

            tis = {}
            for jj in range(LA):
                tis[jj] = logits(jj)
            if pre is not None:
                pre()
            for j in range(16):
                if j + LA < 16:
                    tis[j + LA] = logits(j + LA)
                ti = tis.pop(j)
                s = load_slab(slab_src(projW, 0, 8, j * 128), 8)
                slot2 = next_slot()
                mm([(wview(s, k), rhs_list(bb, k)) for k in range(8)], slot2,
                   reads=[("w", s)] + [("bb", k) for k in range(8)])
                if first:
                    tt(m[j], psl(slot2), T[ti][:, :], ALU.mult, [("ps", slot2), ("T", ti)], [("m", j)])
                else:
                    t2 = nxt("t23", 2)
                    tb = T[t2][:, :].bitcast(BF16)[:, 0:TT]
                    tt(tb, psl(slot2), T[ti][:, :], ALU.mult, [("ps", slot2), ("T", ti)], [("T", t2)])
                    tt(m[j], m[j], tb, ALU.add, [("m", j), ("T", t2)], [("m", j)])

        if True:
          for l in layers:
              L = l * PV_L
              rmsnorm_h(L + PV_N1G, pre=(l != layers[0]))
              P.alias([("act", b, i) for b in range(2) for i in range(11)] + [("m", j) for j in range(16)],
                      [("vconv", c) for c in range(8)] + [("bb", c) for c in range(8)])
              def b_gte(c):
                  slot = proj_h(w_in[l], 2048 + c * 128)
                  ti = nxt("t01", 0)
                  act(T[ti][:, :], psl(slot), AF.Sigmoid, [("ps", slot)], [("T", ti)])
                  return ti

              NPE = 16

              def b_diag(c):
                  return (diag_slab([L + PV_CCW + k * 8 + c for k in range(0, 16)]), None)

              def b_val(c, ti):
                  slot = proj_h(w_in[l], 1024 + c * 128)
                  pi = nxt("pad", 0)
                  tt(pads[pi][:, PADL:PADL + TT], psl(slot), T[ti][:, :], ALU.mult,
                     [("ps", slot), ("T", ti), ("padz", pi)], [("pad", pi)])
                  ta = 2 + (c % 2)
                  rd = [("pad", pi), ("padz", pi), "pv"]
                  ts(T[ta][:, :], pads[pi][:, PADL - 30 + NPE:PADL - 30 + NPE + TT],
                     pcol(L + PV_CCW + NPE * 8 + c), None, ALU.mult, None, rd, [("T", ta)])
                  for k in range(NPE + 1, 31):
                      stt(T[ta][:, :], pads[pi][:, PADL - 30 + k:PADL - 30 + k + TT], pcol(L + PV_CCW + k * 8 + c),
                          T[ta][:, :], ALU.mult, ALU.add, rd + [("T", ta)], [("T", ta)])
                  return pi

              def b_conv(c, dd, pi):
                  d0, d1 = dd
                  slot = next_slot()
                  mm([(wview(d0, k), rhs_shift(pads[pi], PADL - 30 + k)) for k in range(16)], slot,
                     reads=[("w", d0), ("pad", pi), ("padz", pi)], first=True, last=True)
                  ta = 2 + (c % 2)
                  stt(vconv[c], psl(slot), pcol(L + PV_CCB + c), T[ta][:, :], ALU.add, ALU.add,
                      [("ps", slot), "pv", ("T", ta)], [("vconv", c)])
                  act(bb[c], vconv[c], AF.Square, [("vconv", c)], [("bb", c)])

              ti_cur = b_gte(0)
              dd_cur = b_diag(0)
              pi_cur = b_val(0, ti_cur)
              for c in range(8):
                  if c < 7:
                      ti_n = b_gte(c + 1)
                      dd_n = b_diag(c + 1)
                  b_conv(c, dd_cur, pi_cur)
                  if c < 7:
                      pi_cur = b_val(c + 1, ti_n)
                      dd_cur = dd_n
              s_mean = next_slot()
              for c in range(8):
                  mm([(ones_ln[:, :], lambda t0, tn, c=c: vconv[c][:, t0:t0 + tn])],
                     s_mean, reads=[("vconv", c), "ones_ln"], first=(c == 0), last=(c == 7))
              s_msq = next_slot()
              for c in range(8):
                  mm([(ones_ln_b[:, :], lambda t0, tn, c=c: bb[c][:, t0:t0 + tn])],
                     s_msq, reads=[("bb", c), "ones_ln"], first=(c == 0), last=(c == 7))
              act(T[3][:, :], psl(s_mean), AF.Identity, [("ps", s_mean)], [("T", 3)])
              tt(T[2][:, :], T[3][:, :], T[3][:, :], ALU.mult, [("T", 3)], [("T", 2)])
              tt(T[4][:, :], psl(s_msq), T[2][:, :], ALU.subtract, [("ps", s_msq), ("T", 2)], [("T", 4)])
              act(T[4][:, :], T[4][:, :], AF.Sqrt, [("T", 4), "epsc"], [("T", 4)], bias=epsc[:, 0:1])
              P.op("dve", lambda e: e.reciprocal(out=T[4][:, :], in_=T[4][:, :]),
                   reads=[("T", 4)], writes=[("T", 4)])
              def ln_apply():
                  for c in (6, 7):
                      P.op("pool", lambda e, c=c: e.tensor_tensor(out=vconv[c], in0=vconv[c], in1=T[3][:, :],
                                                                   op=ALU.subtract),
                           reads=[("vconv", c), ("T", 3)], writes=[("vconv", c)])
                      P.op("pool", lambda e, c=c: e.tensor_tensor(out=vconv[c], in0=vconv[c], in1=T[4][:, :],
                                                                   op=ALU.mult),
                           reads=[("vconv", c), ("T", 4)], writes=[("vconv", c)])
                  for c in range(8):
                      if c < 6:
                          tt(vconv[c], vconv[c], T[3][:, :], ALU.subtract, [("vconv", c), ("T", 3)], [("vconv", c)])
                          tt(vconv[c], vconv[c], T[4][:, :], ALU.mult, [("vconv", c), ("T", 4)], [("vconv", c)])
                      act(bb[c], vconv[c], AF.Silu, [("vconv", c), "pv"], [("bb", c)],
                          bias=pcol(L + PV_LNB + c), scale=pcol(L + PV_LNG + c))
                  P.alias([("vconv", c) for c in range(8)], [("m", j) for j in range(16)])
              gate_branch(1, conf_proj[l], True, L, l, pre=ln_apply)
              def a_u(c):
                  slot = proj_h(w_in[l], c * 128)
                  pi = nxt("pad", 0)
                  act(pads[pi][:, PADL:PADL + TT], psl(slot), AF.Identity, [("ps", slot), ("padz", pi)], [("pad", pi)])
                  return pi

              def a_win(c, pi, t2):
                  g, cc = c // 2, c % 2
                  w = 2 << g
                  pooled = T[t2][:, :].bitcast(BF16)
                  slot2 = next_slot()
                  mm([(poolI[:, 2 * g + (1 if k > 0 else 0), :], rhs_shift(pads[pi], PADL - k)) for k in range(w)],
                     slot2, reads=[("pad", pi), ("padz", pi), "poolI"])
                  pc = pooled[:, cc * TT:(cc + 1) * TT]
                  act(pc, psl(slot2), AF.Identity, [("ps", slot2)], [("T", t2)])
                  P.op("dve", lambda e, pi=pi: e.tensor_copy(out=small_u[:, :], in_=pads[pi][:, PADL:PADL + 16]),
                       reads=[("pad", pi)], writes=["small_u"])
                  tt(small[:, :], ps[:, SBASE[slot2]:SBASE[slot2] + 16], small_u[:, :], ALU.add,
                     [("ps", slot2), "small_u"], ["small"])
                  tt(small[:, :], small[:, :], pv[:, PV_FAC + g * 16:PV_FAC + g * 16 + 16], ALU.mult,
                     ["small", "pv"], ["small"])
                  tt(small[:, :], small[:, :], small_u[:, :], ALU.subtract, ["small", "small_u"], ["small"])
                  P.op("dve", lambda e, pc=pc: e.tensor_copy(out=pc[:, 0:16], in_=small[:, :]),
                       reads=["small"], writes=[("T", t2)])

              def a_mix(g, t2):
                  pooled = T[t2][:, :].bitcast(BF16)
                  for mmi in range(2):
                      src = pool_w[l, g][:, mmi * 128:(mmi + 1) * 128].rearrange("(k p) m -> p k m", p=128)
                      s = load_slab(src, 2)
                      slot = next_slot()
                      mm([(wview(s, k), (lambda t0, tn, k=k, pooled=pooled: pooled[:, k * TT + t0:k * TT + t0 + tn])) for k in range(2)],
                         slot, reads=[("w", s), ("T", t2)])
                      act(bb[2 * g + mmi], psl(slot), AF.Identity, [("ps", slot), "pv"], [("bb", 2 * g + mmi)],
                          scale=pcol(L + PV_PSC + 2 * g + mmi))

              t2s = [nxt("t23", 2) for _ in range(4)]
              pi_cur = a_u(0)
              pend = None
              for c in range(8):
                  if c < 7:
                      pi_n = a_u(c + 1)
                  a_win(c, pi_cur, t2s[c // 2])
                  if pend is not None:
                      a_mix(*pend)
                      pend = None
                  if c % 2 == 1:
                      pend = (c // 2, t2s[c // 2])
                  if c < 7:
                      pi_cur = pi_n
              a_mix(*pend)
              gate_branch(0, pool_proj[l], False, L, l)
              def c_cg(c):
                  slot = proj_h(w_in[l], 4096 + c * 128)
                  ti = nxt("t01", 0)
                  act(T[ti][:, :], psl(slot), AF.Identity, [("ps", slot)], [("T", ti)])
                  return ti

              def c_hs(c, ti):
                  slot = proj_h(w_in[l], 5120 + c * 128)
                  t2 = nxt("t23", 2)
                  tt(T[t2][:, :], psl(slot), T[ti][:, :], ALU.mult, [("ps", slot), ("T", ti)], [("T", t2)])
                  wc = L + PV_SCW + c
                  act(T[ti][:, :], T[t2][:, :], AF.Identity, [("T", t2), "pv"], [("T", ti)], scale=pcol(wc + 16))
                  stt(T[ti][:, 1:TT], T[t2][:, 0:TT - 1], pcol(wc + 8), T[ti][:, 1:TT], ALU.mult, ALU.add,
                      [("T", t2), ("T", ti), "pv"], [("T", ti)])
                  stt(T[ti][:, 2:TT], T[t2][:, 0:TT - 2], pcol(wc), T[ti][:, 2:TT], ALU.mult, ALU.add,
                      [("T", t2), ("T", ti), "pv"], [("T", ti)])
                  return ti

              def c_bg(c, ti):
                  slot = proj_h(w_in[l], 3072 + c * 128)
                  tt(bb[c], psl(slot), T[ti][:, :], ALU.mult, [("ps", slot), ("T", ti)], [("bb", c)])

              ti_cur = c_hs(0, c_cg(0))
              for c in range(8):
                  if c < 7:
                      ti_n = c_cg(c + 1)
                  c_bg(c, ti_cur)
                  if c < 7:
                      ti_cur = c_hs(c + 1, ti_n)
              gate_branch(2, sconv_proj[l], False, L, l)
              for j in range(16):
                  s = load_slab(slab_src(w_o[l], 0, 16, j * 128), 16)
                  slot = next_slot()
                  mm([(wview(s, k), rhs_list(m, k)) for k in range(16)], slot,
                     reads=[("w", s)] + [("m", k) for k in range(16)])
                  tt(x[:, j, :], psl(slot), x[:, j, :], ALU.add, [("ps", slot), ("x", j)], [("x", j)])
                  sq_acc(j)
              rmsnorm_h(L + PV_N2G, pre=True)
              P.alias([("m", j) for j in range(16)] + [("bb", c) for c in range(8)],
                      [("act", b, i) for b in range(2) for i in range(11)])

              def conv3_evac(slot, dst, key, wc):
                  act(dst, psl(slot), AF.Identity, [("ps", slot), "pv"], [key], scale=pcol(wc + 2 * 86))
                  stt(dst[:, 1:TT], ps[:, SBASE[slot]:SBASE[slot] + TT - 1], pcol(wc + 86), dst[:, 1:TT],
                      ALU.mult, ALU.add, [("ps", slot), key, "pv"], [key])
                  stt(dst[:, 2:TT], ps[:, SBASE[slot]:SBASE[slot] + TT - 2], pcol(wc), dst[:, 2:TT],
                      ALU.mult, ALU.add, [("ps", slot), key, "pv"], [key])

              def ffn_up_group(gi):
                  g0, ng = FGROUPS[gi]
                  b = gi % 2
                  for ii in range(ng):
                      i = g0 + ii
                      tg = 0 if ii % 2 == 0 else 2
                      slot = proj_h(ffn_up[l], i * 128)
                      conv3_evac(slot, T[tg][:, :], ("T", tg), L + PV_FCW + i)
                      act(T[tg][:, :], T[tg][:, :], AF.Silu, [("T", tg)], [("T", tg)])
                      slot = proj_h(ffn_up[l], (NFC + i) * 128)
                      conv3_evac(slot, T[tg + 1][:, :], ("T", tg + 1), L + PV_FCW + NFC + i)
                      tt(actc(b, ii), T[tg][:, :], T[tg + 1][:, :], ALU.mult,
                         [("T", tg), ("T", tg + 1)], [("act", b, ii)])

              def ffn_down_group(gi):
                  g0, ng = FGROUPS[gi]
                  b = gi % 2
                  for j in range(16):
                      s = load_slab(slab_src(ffn_down[l], g0 * 128, ng, j * 128), ng)
                      slot = next_slot()
                      mm([(wview(s, k), (lambda t0, tn, k=k: actc(b, k)[:, t0:t0 + tn])) for k in range(ng)], slot,
                         reads=[("w", s)] + [("act", b, k) for k in range(ng)])
                      tt(x[:, j, :], psl(slot), x[:, j, :], ALU.add, [("ps", slot), ("x", j)], [("x", j)])
                      if gi == 3:
                          sq_acc(j)
              ffn_up_group(0)
              ffn_up_group(1)
              ffn_down_group(0)
              ffn_up_group(2)
              ffn_down_group(1)
              ffn_up_group(3)
              ffn_down_group(2)
              ffn_down_group(3)


        if final:
            rms_stats(4, pre=True)
            for c in range(16):
                stt(x[:, c, :], x[:, c, :], pcol(PV_FG + c), T[4][:, :], ALU.mult, ALU.mult,
                    [("x", c), ("T", 4), "pv"], [("x", c)])
        for q in range(8):
            P.op("sp", lambda e, q=q: e.dma_start(out=yout[:, 2 * q:2 * q + 2, :], in_=x[:, 2 * q:2 * q + 2, :]),
                 reads=[("x", j) for j in range(2 * q, 2 * q + 2)], sem="out", inc=16)
        P.streams["sp"].items.append(("wait", "out", P.count["out"]))

        def run(eng_name):
            def f(e):
                for it in P.streams[eng_name].items:
                    if it[0] == "wait":
                        e.wait_ge(sems[it[1]], it[2])
                    else:
                        ins = it[1](e)
                        ins.then_inc(sems[it[2]], it[3])
            return f

        block.tensor(run("pe"))
        block.scalar(run("act"))
        block.vector(run("dve"))
        block.gpsimd(run("pool"))
        block.sync(run("sp"))
    return nc


_WNAMES = ["w_in", "pool_w", "pool_proj", "conf_proj", "sconv_proj", "w_o", "ffn_up", "ffn_down"]


def _pack_pv(inp, half):
    pv = np.zeros((128, NPV), np.float32)

    def put(col, vec):
        v = np.asarray(vec, np.float32).reshape(-1, 128).T
        pv[:, col:col + v.shape[1]] = v

    for l in range(2):
        L = l * PV_L
        put(L + PV_N1G, inp["norm1_g"][l])
        put(L + PV_GB, inp["gate_b"][l])
        put(L + PV_PSC, inp["pool_scale"][l])
        put(L + PV_CCW, inp["conf_conv_w"][l].reshape(-1))
        put(L + PV_CCB, inp["conf_conv_b"][l])
        put(L + PV_LNG, inp["conf_ln_g"][l])
        put(L + PV_LNB, inp["conf_ln_b"][l])
        put(L + PV_SCW, inp["sconv_w"][l].reshape(-1))
        put(L + PV_N2G, inp["norm2_g"][l])
        put(L + PV_FCW, inp["ffn_conv_w"][l].reshape(-1))
    put(PV_FG, inp["final_g"])
    fac = np.ones((4, 16), np.float32)
    if half == 0:
        for g in range(4):
            w = 2 << g
            for t in range(w - 1):
                fac[g, t] = w / (t + 1.0)
    pv[:, PV_FAC:PV_FAC + 64] = fac.reshape(1, 64)
    pv[:, PV_ID:PV_ID + 128] = np.eye(128, dtype=np.float32)
    return pv


_PROG = {}


def _get_prog(layers, final):
    key = (tuple(layers), final)
    if key not in _PROG:
        _PROG[key] = build_program(list(layers), final)
    return _PROG[key]


def _tok0(half):
    return 0 if half == 0 else SEQ - TT


def _run(prog, xTs, pvs, wts):
    in_maps = []
    for i in range(8):
        d = {"xT": xTs[i], "pv": pvs[i]}
        d.update(wts)
        in_maps.append(d)
    res = run_bass_kernel_spmd(prog, in_maps, core_ids=list(range(8)))
    return [np.asarray(r["yT"]) for r in res.results]


def kernel(**inputs):
    inp = {k: np.asarray(v) for k, v in inputs.items()}
    x = inp["x"].astype(np.float32, copy=False)
    wts = {n: np.ascontiguousarray(inp[n], dtype=np.float32) for n in _WNAMES}
    xTs, pvs = [], []
    for i in range(8):
        b, half = i // 2, i % 2
        t0 = _tok0(half)
        xt = x[b, t0:t0 + TT, :].T.reshape(NDC, 128, TT).transpose(1, 0, 2)
        xTs.append(np.ascontiguousarray(xt))
        pvs.append(_pack_pv(inp, half))
    ys = _run(_get_prog((0, 1), True), xTs, pvs, wts)
    out = np.empty((4, SEQ, D), np.float32)
    for i in range(8):
        b, half = i // 2, i % 2
        y = ys[i].transpose(1, 0, 2).reshape(D, TT).T
        if half == 0:
            out[b, 0:TT, :] = y
        else:
            out[b, TT:SEQ, :] = y[HALO:, :]
    return out
```

```python
import contextlib
import numpy as np
import concourse.bass as bass
import concourse.mybir as mybir
from concourse.bass_utils import run_bass_kernel_spmd

F32 = mybir.dt.float32
BF16 = mybir.dt.bfloat16
F32R = mybir.dt.float32r
ALU = mybir.AluOpType
AF = mybir.ActivationFunctionType

D = 2048
NDC = 16
TT = 1056
TILES = [(0, 272), (272, 512), (784, 272)]
HALO = 64
SEQ = 2048
EPS = 1e-6
NSLOT = 6
PADL = 32
DFF = 5504
NFC = 43
FGROUPS = [(0, 11), (11, 11), (22, 11), (33, 10)]

PV_N1G = 0
PV_GB = 16
PV_PSC = 64
PV_CCW = 72
PV_CCB = 320
PV_LNG = 328
PV_LNB = 336
PV_SCW = 344
PV_N2G = 368
PV_FCW = 384
PV_L = 642
PV_FG = 2 * PV_L
PV_FAC = PV_FG + 16
PV_ID = PV_FAC + 64
NPV = PV_ID + 128


class Stream:
    def __init__(self):
        self.items = []
        self.waited = {}


class Plan:
    def __init__(self):
        self.streams = {k: Stream() for k in ("pe", "act", "dve", "pool", "sp")}
        self.count = {}
        self.lastw = {}
        self.readers = {}

    def _need(self, reads, writes):
        need = {}

        def add(tok):
            if tok is None:
                return
            s, v = tok
            if need.get(s, 0) < v:
                need[s] = v

        for k in reads:
            add(self.lastw.get(k))
        for k in writes:
            add(self.lastw.get(k))
            for s, v in self.readers.get(k, {}).items():
                add((s, v))
        return need

    def op(self, eng, fn, reads=(), writes=(), sem=None, inc=1):
        st = self.streams[eng]
        if eng in ("act", "dve"):
            extra = [("psr", k[1]) for k in reads if isinstance(k, tuple) and k[0] == "ps"]
            if extra:
                writes = list(writes) + extra
        need = self._need(reads, writes)
        for s, v in need.items():
            if st.waited.get(s, 0) < v:
                st.waited[s] = v
                st.items.append(("wait", s, v))
        if sem is None:
            sem = eng
        self.count[sem] = self.count.get(sem, 0) + inc
        tok = (sem, self.count[sem])
        st.items.append(("op", fn, sem, inc))
        for k in reads:
            r = self.readers.setdefault(k, {})
            if r.get(sem, 0) < tok[1]:
                r[sem] = tok[1]
        for k in writes:
            self.lastw[k] = tok
            self.readers[k] = {}
        return tok

    def wait_all(self, eng, keys):
        st = self.streams[eng]
        need = self._need(keys, keys)
        for s, v in need.items():
            if st.waited.get(s, 0) < v:
                st.waited[s] = v
                st.items.append(("wait", s, v))

    def alias(self, old, new):
        acc = {}
        for k in old:
            t = self.lastw.get(k)
            if t is not None and acc.get(t[0], 0) < t[1]:
                acc[t[0]] = t[1]
            for s, v in self.readers.get(k, {}).items():
                if acc.get(s, 0) < v:
                    acc[s] = v
        for k in new:
            r = dict(acc)
            t = self.lastw.get(k)
            if t is not None and r.get(t[0], 0) < t[1]:
                r[t[0]] = t[1]
            for s, v in self.readers.get(k, {}).items():
                if r.get(s, 0) < v:
                    r[s] = v
            self.lastw[k] = None
            self.readers[k] = r

    def emit(self, eng, e, sems):
        for it in self.streams[eng].items:
            if it[0] == "wait":
                e.wait_ge(sems[it[1]], it[2])
            else:
                ins = it[1](e)
                ins.then_inc(sems[it[2]], it[3])


def build_program(layers, final):
    nc = bass.Bass("TRN2", target_bir_lowering=False)
    xin = nc.dram_tensor("xT", [128, NDC, TT], F32, kind="ExternalInput").ap()
    pvd = nc.dram_tensor("pv", [128, NPV], F32, kind="ExternalInput").ap()
    w_in = nc.dram_tensor("w_in", [2, D, 12288], F32, kind="ExternalInput").ap()
    pool_w = nc.dram_tensor("pool_w", [2, 4, 256, 256], F32, kind="ExternalInput").ap()
    pool_proj = nc.dram_tensor("pool_proj", [2, 1024, D], F32, kind="ExternalInput").ap()
    conf_proj = nc.dram_tensor("conf_proj", [2, 1024, D], F32, kind="ExternalInput").ap()
    sconv_proj = nc.dram_tensor("sconv_proj", [2, 1024, D], F32, kind="ExternalInput").ap()
    w_o = nc.dram_tensor("w_o", [2, D, D], F32, kind="ExternalInput").ap()
    ffn_up = nc.dram_tensor("ffn_up", [2, D, 2 * DFF], F32, kind="ExternalInput").ap()
    ffn_down = nc.dram_tensor("ffn_down", [2, DFF, D], F32, kind="ExternalInput").ap()
    yout = nc.dram_tensor("yT", [128, NDC, TT], F32, kind="ExternalOutput").ap()

    with contextlib.ExitStack() as ctx:
        def sb(name, shape, dt):
            return ctx.enter_context(nc.sbuf_tensor(name, shape, dt))

        x = sb("x", [128, NDC, TT], F32)
        h = sb("h", [128, NDC, TT], BF16)
        big = sb("big", [128, 12 * TT], F32)
        wsl = [sb(f"ws{i}", [128, 2048], BF16) for i in range(NSLOT)]
        pv = sb("pvs", [128, NPV], F32)
        T = [sb(f"T{i}", [128, TT], F32) for i in range(5)]
        pads = [sb(f"pad{i}", [128, PADL + TT], BF16) for i in range(2)]
        ones_rms = sb("ones_rms", [128, 128], F32)
        ones_ln = sb("ones_ln", [128, 128], F32)
        ones_rms_b = sb("ones_rms_b", [128, 128], BF16)
        ones_ln_b = sb("ones_ln_b", [128, 128], BF16)
        poolI = sb("poolI", [128, 8, 128], BF16)
        small = sb("small", [128, 16], F32)
        small_u = sb("small_u", [128, 16], F32)
        epsc = sb("epsc", [128, 8], F32)
        ps = ctx.enter_context(nc.psum_tensor("ps", [128, 4096], F32))

        sem_names = (["pe", "act", "dve", "pool", "init", "out"] + [f"w{i}" for i in range(NSLOT)]
                     + [f"x{q}" for q in range(8)])
        sems = {n: ctx.enter_context(nc.semaphore(n)) for n in sem_names}
        block = ctx.enter_context(nc.Block())

        P = Plan()
        SBASE = [240, 1776]
        st = {"slot": 0, "slab": 0, "t01": 0, "t23": 0, "pad": 0}

        ident = pv[:, PV_ID:PV_ID + 128]
        vconv = [big[:, c * TT:(c + 1) * TT] for c in range(8)]
        mflat = big[:, 0:8 * TT].bitcast(BF16)
        m = [mflat[:, j * TT:(j + 1) * TT] for j in range(16)]
        bbflat = big[:, 8 * TT:12 * TT].bitcast(BF16)
        bb = [bbflat[:, c * TT:(c + 1) * TT] for c in range(8)]
        actflat = [big[:, 0:5808].bitcast(BF16), big[:, 5808:11616].bitcast(BF16)]

        def actc(b, i):
            return actflat[b][:, i * TT:(i + 1) * TT]

        def pcol(c):
            return pv[:, c:c + 1]

        def next_slot():
            s = st["slot"]
            st["slot"] = 1 - s
            return s

        def psl(s):
            return ps[:, SBASE[s]:SBASE[s] + TT]

        def nxt(name, base):
            i = st[name]
            st[name] = 1 - i
            return base + i

        def load_slab(src, kc):
            s = st["slab"] % NSLOT
            st["slab"] += 1
            dst = wsl[s][:, 0:kc * 128].rearrange("p (k m) -> p k m", m=128)
            P.op("pool", lambda e: e.dma_start(out=dst, in_=src),
                 reads=(), writes=(("w", s),), sem=f"w{s}", inc=16)
            return s

        def wview(s, k):
            return wsl[s][:, k * 128:(k + 1) * 128]

        def slab_src(W2d, r0, kc, c0):
            return W2d[r0:r0 + kc * 128, c0:c0 + 128].rearrange("(k p) m -> p k m", p=128)

        def diag_slab(cols):
            s = st["slab"] % NSLOT
            st["slab"] += 1
            for i, col in enumerate(cols):
                P.op("act", lambda e, i=i, col=col: e.activation(
                    out=wview(s, i), in_=ident, func=AF.Identity, scale=pcol(col)),
                    reads=("pv",), writes=(("w", s),))
            return s

        def mm(parts, slot, reads, first=True, last=True, fp32r=False):
            n = len(parts)

            def fn(e):
                ins = None
                for i, (lt, rf) in enumerate(parts):
                    for (t0, tn) in TILES:
                        o = ps[:, SBASE[slot] + t0:SBASE[slot] + t0 + tn]
                        ins = e.matmul(out=o, lhsT=lt, rhs=rf(t0, tn),
                                       start=(first and i == 0), stop=(last and i == n - 1))
                return ins
            return P.op("pe", fn, reads=reads, writes=(("ps", slot),))

        def rhs_chunks(buf3, c):
            return lambda t0, tn: buf3[:, c, t0:t0 + tn]

        def rhs_list(lst, c):
            return lambda t0, tn: lst[c][:, t0:t0 + tn]

        def rhs_shift(padbuf, off):
            return lambda t0, tn: padbuf[:, off + t0:off + t0 + tn]

        def proj_h(W2d, col0):
            s = load_slab(slab_src(W2d, 0, 16, col0), 16)
            slot = next_slot()
            if st.get("perk"):
                st["perk"] = False
                for k in range(16):
                    mm([(wview(s, k), rhs_chunks(h, k))], slot, reads=[("w", s), ("h", k)],
                       first=(k == 0), last=(k == 15))
                return slot
            mm([(wview(s, k), rhs_chunks(h, k)) for k in range(16)], slot,
               reads=[("w", s)] + [("h", k) for k in range(16)])
            return slot

        def act(out, in_, func, reads, writes, bias=None, scale=None):
            kw = {}
            if bias is not None:
                kw["bias"] = bias
            if scale is not None:
                kw["scale"] = scale
            return P.op("act", lambda e: e.activation(out=out, in_=in_, func=func, **kw),
                        reads=reads, writes=writes)

        def tt(out, a, b, op, reads, writes):
            return P.op("dve", lambda e: e.tensor_tensor(out=out, in0=a, in1=b, op=op),
                        reads=reads, writes=writes)

        def ts(out, a, s1, s2, op0, op1, reads, writes):
            if op1 is None:
                return P.op("dve", lambda e: e.tensor_scalar(out=out, in0=a, scalar1=s1, scalar2=None, op0=op0),
                            reads=reads, writes=writes)
            return P.op("dve", lambda e: e.tensor_scalar(out=out, in0=a, scalar1=s1, scalar2=s2, op0=op0, op1=op1),
                        reads=reads, writes=writes)

        def stt(out, a, sc, b, op0, op1, reads, writes):
            return P.op("dve", lambda e: e.scalar_tensor_tensor(out=out, in0=a, scalar=sc, in1=b, op0=op0, op1=op1),
                        reads=reads, writes=writes)

        P.op("sp", lambda e: e.dma_start(out=pv[:, :], in_=pvd[:, :]), writes=("pv",), sem="init", inc=16)
        for q in range(8):
            P.op("sp", lambda e, q=q: e.dma_start(out=x[:, 2 * q:2 * q + 2, :], in_=xin[:, 2 * q:2 * q + 2, :]),
                 writes=[("x", j) for j in range(2 * q, 2 * q + 2)], sem=f"x{q}", inc=16)

        P.op("dve", lambda e: e.memset(ones_rms[:, :], 1.0 / D), writes=("ones_rms",))
        P.op("dve", lambda e: e.memset(epsc[:, :], EPS), writes=("epsc",))
        P.op("dve", lambda e: e.memset(ones_rms_b[:, :], 1.0 / D), writes=("ones_rms",))
        P.op("dve", lambda e: e.memset(ones_ln_b[:, :], 1.0 / 1024.0), writes=("ones_ln",))
        P.op("dve", lambda e: e.memset(ones_ln[:, :], 1.0 / 1024.0), writes=("ones_ln",))
        for i in range(2):
            P.op("dve", lambda e, i=i: e.memset(pads[i][:, 0:PADL], 0.0), writes=(("padz", i),))
        for g in range(4):
            w = 2 << g
            ts(poolI[:, 2 * g, :], ident, 1.0 / w - 1.0, None, ALU.mult, None, ["pv"], ["poolI"])
            ts(poolI[:, 2 * g + 1, :], ident, 1.0 / w, None, ALU.mult, None, ["pv"], ["poolI"])

        def sq_acc(c):
            ti = nxt("t01", 0)
            act(T[ti][:, :], x[:, c, :], AF.Square, [("x", c)], [("T", ti)])
            if c == 0:
                P.op("dve", lambda e, ti=ti: e.tensor_copy(out=T[4][:, :], in_=T[ti][:, :]),
                     reads=[("T", ti)], writes=[("T", 4)])
            else:
                tt(T[4][:, :], T[4][:, :], T[ti][:, :], ALU.add, [("T", 4), ("T", ti)], [("T", 4)])

        def rms_stats(rstd_T, pre=False):
            if not pre:
                for c in range(16):
                    sq_acc(c)
            slot = next_slot()
            mm([(ones_rms[:, :], lambda t0, tn: T[4][:, t0:t0 + tn])], slot, reads=[("T", 4), "ones_rms"])
            act(T[rstd_T][:, :], psl(slot), AF.Sqrt, [("ps", slot), "epsc"], [("T", rstd_T)], bias=epsc[:, 0:1])
            P.op("dve", lambda e: e.reciprocal(out=T[rstd_T][:, :], in_=T[rstd_T][:, :]),
                 reads=[("T", rstd_T)], writes=[("T", rstd_T)])

        def rmsnorm_h(gcol, pre=False):
            rms_stats(4, pre)
            for c in range(16):
                stt(h[:, c, :], x[:, c, :], pcol(gcol + c), T[4][:, :], ALU.mult, ALU.mult,
                    [("x", c), ("T", 4), "pv"], [("h", c)])
            st["perk"] = True

        def gate_branch(br, projW, first, L, l, pre=None):
            LA = 2 if first else 1
            tbufs = [0, 1, 2] if first else [0, 1]

            def logits(j):
                slot = proj_h(w_in[l], 6144 + br * 2048 + j * 128)
                ti = tbufs[j % len(tbufs)]
                act(T[ti][:, :], psl(slot), AF.Sigmoid, [("ps", slot), "pv"], [("T", ti)],
                    bias=pcol(L + PV_GB + br * 16 + j))
                return ti

            tis = {}
            for jj in range(LA):
                tis[jj] = logits(jj)
            if pre is not None:
                pre()
            for j in range(16):
                if j + LA < 16:
                    tis[j + LA] = logits(j + LA)
                ti = tis.pop(j)
                s = load_slab(slab_src(projW, 0, 8, j * 128), 8)
                slot2 = next_slot()
                mm([(wview(s, k), rhs_list(bb, k)) for k in range(8)], slot2,
                   reads=[("w", s)] + [("bb", k) for k in range(8)])
                if first:
                    tt(m[j], psl(slot2), T[ti][:, :], ALU.mult, [("ps", slot2), ("T", ti)], [("m", j)])
                else:
                    t2 = nxt("t23", 2)
                    tb = T[t2][:, :].bitcast(BF16)[:, 0:TT]
                    tt(tb, psl(slot2), T[ti][:, :], ALU.mult, [("ps", slot2), ("T", ti)], [("T", t2)])
                    tt(m[j], m[j], tb, ALU.add, [("m", j), ("T", t2)], [("m", j)])

        if True:
          for l in layers:
              L = l * PV_L
              rmsnorm_h(L + PV_N1G, pre=(l != layers[0]))
              P.alias([("act", b, i) for b in range(2) for i in range(11)] + [("m", j) for j in range(16)],
                      [("vconv", c) for c in range(8)] + [("bb", c) for c in range(8)])
              def b_gte(c):
                  slot = proj_h(w_in[l], 2048 + c * 128)
                  ti = nxt("t01", 0)
                  act(T[ti][:, :], psl(slot), AF.Sigmoid, [("ps", slot)], [("T", ti)])
                  return ti

              NPE = 16

              def b_diag(c):
                  return (diag_slab([L + PV_CCW + k * 8 + c for k in range(0, 16)]), None)

              def b_val(c, ti):
                  slot = proj_h(w_in[l], 1024 + c * 128)
                  pi = nxt("pad", 0)
                  tt(pads[pi][:, PADL:PADL + TT], psl(slot), T[ti][:, :], ALU.mult,
                     [("ps", slot), ("T", ti), ("padz", pi)], [("pad", pi)])
                  ta = 2 + (c % 2)
                  rd = [("pad", pi), ("padz", pi), "pv"]
                  ts(T[ta][:, :], pads[pi][:, PADL - 30 + NPE:PADL - 30 + NPE + TT],
                     pcol(L + PV_CCW + NPE * 8 + c), None, ALU.mult, None, rd, [("T", ta)])
                  for k in range(NPE + 1, 31):
                      stt(T[ta][:, :], pads[pi][:, PADL - 30 + k:PADL - 30 + k + TT], pcol(L + PV_CCW + k * 8 + c),
                          T[ta][:, :], ALU.mult, ALU.add, rd + [("T", ta)], [("T", ta)])
                  return pi

              def b_conv(c, dd, pi):
                  d0, d1 = dd
                  slot = next_slot()
                  mm([(wview(d0, k), rhs_shift(pads[pi], PADL - 30 + k)) for k in range(16)], slot,
                     reads=[("w", d0), ("pad", pi), ("padz", pi)], first=True, last=True)
                  ta = 2 + (c % 2)
                  stt(vconv[c], psl(slot), pcol(L + PV_CCB + c), T[ta][:, :], ALU.add, ALU.add,
                      [("ps", slot), "pv", ("T", ta)], [("vconv", c)])
                  act(bb[c], vconv[c], AF.Square, [("vconv", c)], [("bb", c)])

              ti_cur = b_gte(0)
              dd_cur = b_diag(0)
              pi_cur = b_val(0, ti_cur)
              for c in range(8):
                  if c < 7:
                      ti_n = b_gte(c + 1)
                      dd_n = b_diag(c + 1)
                  b_conv(c, dd_cur, pi_cur)
                  if c < 7:
                      pi_cur = b_val(c + 1, ti_n)
                      dd_cur = dd_n
              s_mean = next_slot()
              for c in range(8):
                  mm([(ones_ln[:, :], lambda t0, tn, c=c: vconv[c][:, t0:t0 + tn])],
                     s_mean, reads=[("vconv", c), "ones_ln"], first=(c == 0), last=(c == 7))
              s_msq = next_slot()
              for c in range(8):
                  mm([(ones_ln_b[:, :], lambda t0, tn, c=c: bb[c][:, t0:t0 + tn])],
                     s_msq, reads=[("bb", c), "ones_ln"], first=(c == 0), last=(c == 7))
              act(T[3][:, :], psl(s_mean), AF.Identity, [("ps", s_mean)], [("T", 3)])
              tt(T[2][:, :], T[3][:, :], T[3][:, :], ALU.mult, [("T", 3)], [("T", 2)])
              tt(T[4][:, :], psl(s_msq), T[2][:, :], ALU.subtract, [("ps", s_msq), ("T", 2)], [("T", 4)])
              act(T[4][:, :], T[4][:, :], AF.Sqrt, [("T", 4), "epsc"], [("T", 4)], bias=epsc[:, 0:1])
              P.op("dve", lambda e: e.reciprocal(out=T[4][:, :], in_=T[4][:, :]),
                   reads=[("T", 4)], writes=[("T", 4)])
              def ln_apply():
                  for c in range(8):
                      tt(vconv[c], vconv[c], T[3][:, :], ALU.subtract, [("vconv", c), ("T", 3)], [("vconv", c)])
                      tt(vconv[c], vconv[c], T[4][:, :], ALU.mult, [("vconv", c), ("T", 4)], [("vconv", c)])
                      act(bb[c], vconv[c], AF.Silu, [("vconv", c), "pv"], [("bb", c)],
                          bias=pcol(L + PV_LNB + c), scale=pcol(L + PV_LNG + c))
                  P.alias([("vconv", c) for c in range(8)], [("m", j) for j in range(16)])
              gate_branch(1, conf_proj[l], True, L, l, pre=ln_apply)
              def a_u(c):
                  slot = proj_h(w_in[l], c * 128)
                  pi = nxt("pad", 0)
                  act(pads[pi][:, PADL:PADL + TT], psl(slot), AF.Identity, [("ps", slot), ("padz", pi)], [("pad", pi)])
                  return pi

              def a_win(c, pi, t2):
                  g, cc = c // 2, c % 2
                  w = 2 << g
                  pooled = T[t2][:, :].bitcast(BF16)
                  slot2 = next_slot()
                  mm([(poolI[:, 2 * g + (1 if k > 0 else 0), :], rhs_shift(pads[pi], PADL - k)) for k in range(w)],
                     slot2, reads=[("pad", pi), ("padz", pi), "poolI"])
                  pc = pooled[:, cc * TT:(cc + 1) * TT]
                  act(pc, psl(slot2), AF.Identity, [("ps", slot2)], [("T", t2)])
                  P.op("dve", lambda e, pi=pi: e.tensor_copy(out=small_u[:, :], in_=pads[pi][:, PADL:PADL + 16]),
                       reads=[("pad", pi)], writes=["small_u"])
                  tt(small[:, :], ps[:, SBASE[slot2]:SBASE[slot2] + 16], small_u[:, :], ALU.add,
                     [("ps", slot2), "small_u"], ["small"])
                  tt(small[:, :], small[:, :], pv[:, PV_FAC + g * 16:PV_FAC + g * 16 + 16], ALU.mult,
                     ["small", "pv"], ["small"])
                  tt(small[:, :], small[:, :], small_u[:, :], ALU.subtract, ["small", "small_u"], ["small"])
                  P.op("dve", lambda e, pc=pc: e.tensor_copy(out=pc[:, 0:16], in_=small[:, :]),
                       reads=["small"], writes=[("T", t2)])

              def a_mix(g, t2):
                  pooled = T[t2][:, :].bitcast(BF16)
                  for mmi in range(2):
                      src = pool_w[l, g][:, mmi * 128:(mmi + 1) * 128].rearrange("(k p) m -> p k m", p=128)
                      s = load_slab(src, 2)
                      slot = next_slot()
                      mm([(wview(s, k), (lambda t0, tn, k=k, pooled=pooled: pooled[:, k * TT + t0:k * TT + t0 + tn])) for k in range(2)],
                         slot, reads=[("w", s), ("T", t2)])
                      act(bb[2 * g + mmi], psl(slot), AF.Identity, [("ps", slot), "pv"], [("bb", 2 * g + mmi)],
                          scale=pcol(L + PV_PSC + 2 * g + mmi))

              t2s = [nxt("t23", 2) for _ in range(4)]
              pi_cur = a_u(0)
              pend = None
              for c in range(8):
                  if c < 7:
                      pi_n = a_u(c + 1)
                  a_win(c, pi_cur, t2s[c // 2])
                  if pend is not None:
                      a_mix(*pend)
                      pend = None
                  if c % 2 == 1:
                      pend = (c // 2, t2s[c // 2])
                  if c < 7:
                      pi_cur = pi_n
              a_mix(*pend)
              gate_branch(0, pool_proj[l], False, L, l)
              def c_cg(c):
                  slot = proj_h(w_in[l], 4096 + c * 128)
                  ti = nxt("t01", 0)
                  act(T[ti][:, :], psl(slot), AF.Identity, [("ps", slot)], [("T", ti)])
                  return ti

              def c_hs(c, ti):
                  slot = proj_h(w_in[l], 5120 + c * 128)
                  t2 = nxt("t23", 2)
                  tt(T[t2][:, :], psl(slot), T[ti][:, :], ALU.mult, [("ps", slot), ("T", ti)], [("T", t2)])
                  wc = L + PV_SCW + c
                  act(T[ti][:, :], T[t2][:, :], AF.Identity, [("T", t2), "pv"], [("T", ti)], scale=pcol(wc + 16))
                  stt(T[ti][:, 1:TT], T[t2][:, 0:TT - 1], pcol(wc + 8), T[ti][:, 1:TT], ALU.mult, ALU.add,
                      [("T", t2), ("T", ti), "pv"], [("T", ti)])
                  stt(T[ti][:, 2:TT], T[t2][:, 0:TT - 2], pcol(wc), T[ti][:, 2:TT], ALU.mult, ALU.add,
                      [("T", t2), ("T", ti), "pv"], [("T", ti)])
                  return ti

              def c_bg(c, ti):
                  slot = proj_h(w_in[l], 3072 + c * 128)
                  tt(bb[c], psl(slot), T[ti][:, :], ALU.mult, [("ps", slot), ("T", ti)], [("bb", c)])

              ti_cur = c_hs(0, c_cg(0))
              for c in range(8):
                  if c < 7:
                      ti_n = c_cg(c + 1)
                  c_bg(c, ti_cur)
                  if c < 7:
                      ti_cur = c_hs(c + 1, ti_n)
              gate_branch(2, sconv_proj[l], False, L, l)
              for j in range(16):
                  s = load_slab(slab_src(w_o[l], 0, 16, j * 128), 16)
                  slot = next_slot()
                  mm([(wview(s, k), rhs_list(m, k)) for k in range(16)], slot,
                     reads=[("w", s)] + [("m", k) for k in range(16)])
                  tt(x[:, j, :], psl(slot), x[:, j, :], ALU.add, [("ps", slot), ("x", j)], [("x", j)])
                  sq_acc(j)
              rmsnorm_h(L + PV_N2G, pre=True)
              P.alias([("m", j) for j in range(16)] + [("bb", c) for c in range(8)],
                      [("act", b, i) for b in range(2) for i in range(11)])

              def conv3_evac(slot, dst, key, wc):
                  act(dst, psl(slot), AF.Identity, [("ps", slot), "pv"], [key], scale=pcol(wc + 2 * 86))
                  stt(dst[:, 1:TT], ps[:, SBASE[slot]:SBASE[slot] + TT - 1], pcol(wc + 86), dst[:, 1:TT],
                      ALU.mult, ALU.add, [("ps", slot), key, "pv"], [key])
                  stt(dst[:, 2:TT], ps[:, SBASE[slot]:SBASE[slot] + TT - 2], pcol(wc), dst[:, 2:TT],
                      ALU.mult, ALU.add, [("ps", slot), key, "pv"], [key])

              def ffn_up_group(gi):
                  g0, ng = FGROUPS[gi]
                  b = gi % 2
                  for ii in range(ng):
                      i = g0 + ii
                      tg = 0 if ii % 2 == 0 else 2
                      slot = proj_h(ffn_up[l], i * 128)
                      conv3_evac(slot, T[tg][:, :], ("T", tg), L + PV_FCW + i)
                      act(T[tg][:, :], T[tg][:, :], AF.Silu, [("T", tg)], [("T", tg)])
                      slot = proj_h(ffn_up[l], (NFC + i) * 128)
                      conv3_evac(slot, T[tg + 1][:, :], ("T", tg + 1), L + PV_FCW + NFC + i)
                      tt(actc(b, ii), T[tg][:, :], T[tg + 1][:, :], ALU.mult,
                         [("T", tg), ("T", tg + 1)], [("act", b, ii)])

              def ffn_down_group(gi):
                  g0, ng = FGROUPS[gi]
                  b = gi % 2
                  for j in range(16):
                      s = load_slab(slab_src(ffn_down[l], g0 * 128, ng, j * 128), ng)
                      slot = next_slot()
                      mm([(wview(s, k), (lambda t0, tn, k=k: actc(b, k)[:, t0:t0 + tn])) for k in range(ng)], slot,
                         reads=[("w", s)] + [("act", b, k) for k in range(ng)])
                      tt(x[:, j, :], psl(slot), x[:, j, :], ALU.add, [("ps", slot), ("x", j)], [("x", j)])
                      if gi == 3:
                          sq_acc(j)
              ffn_up_group(0)
              ffn_up_group(1)
              ffn_down_group(0)
              ffn_up_group(2)
              ffn_down_group(1)
              ffn_up_group(3)
              ffn_down_group(2)
              ffn_down_group(3)


        if final:
            rms_stats(4, pre=True)
            for c in range(16):
                stt(x[:, c, :], x[:, c, :], pcol(PV_FG + c), T[4][:, :], ALU.mult, ALU.mult,
                    [("x", c), ("T", 4), "pv"], [("x", c)])
        for q in range(8):
            P.op("sp", lambda e, q=q: e.dma_start(out=yout[:, 2 * q:2 * q + 2, :], in_=x[:, 2 * q:2 * q + 2, :]),
                 reads=[("x", j) for j in range(2 * q, 2 * q + 2)], sem="out", inc=16)
        P.streams["sp"].items.append(("wait", "out", P.count["out"]))

        def run(eng_name):
            def f(e):
                for it in P.streams[eng_name].items:
                    if it[0] == "wait":
                        e.wait_ge(sems[it[1]], it[2])
                    else:
                        ins = it[1](e)
                        ins.then_inc(sems[it[2]], it[3])
            return f

        block.tensor(run("pe"))
        block.scalar(run("act"))
        block.vector(run("dve"))
        block.gpsimd(run("pool"))
        block.sync(run("sp"))
    return nc


_WNAMES = ["w_in", "pool_w", "pool_proj", "conf_proj", "sconv_proj", "w_o", "ffn_up", "ffn_down"]


def _pack_pv(inp, half):
    pv = np.zeros((128, NPV), np.float32)

    def put(col, vec):
        v = np.asarray(vec, np.float32).reshape(-1, 128).T
        pv[:, col:col + v.shape[1]] = v

    for l in range(2):
        L = l * PV_L
        put(L + PV_N1G, inp["norm1_g"][l])
        put(L + PV_GB, inp["gate_b"][l])
        put(L + PV_PSC, inp["pool_scale"][l])
        put(L + PV_CCW, inp["conf_conv_w"][l].reshape(-1))
        put(L + PV_CCB, inp["conf_conv_b"][l])
        put(L + PV_LNG, inp["conf_ln_g"][l])
        put(L + PV_LNB, inp["conf_ln_b"][l])
        put(L + PV_SCW, inp["sconv_w"][l].reshape(-1))
        put(L + PV_N2G, inp["norm2_g"][l])
        put(L + PV_FCW, inp["ffn_conv_w"][l].reshape(-1))
    put(PV_FG, inp["final_g"])
    fac = np.ones((4, 16), np.float32)
    if half == 0:
        for g in range(4):
            w = 2 << g
            for t in range(w - 1):
                fac[g, t] = w / (t + 1.0)
    pv[:, PV_FAC:PV_FAC + 64] = fac.reshape(1, 64)
    pv[:, PV_ID:PV_ID + 128] = np.eye(128, dtype=np.float32)
    return pv


_PROG = {}


def _get_prog(layers, final):
    key = (tuple(layers), final)
    if key not in _PROG:
        _PROG[key] = build_program(list(layers), final)
    return _PROG[key]


def _tok0(half):
    return 0 if half == 0 else SEQ - TT


def _run(prog, xTs, pvs, wts):
    in_maps = []
    for i in range(8):
        d = {"xT": xTs[i], "pv": pvs[i]}
        d.update(wts)
        in_maps.append(d)
    res = run_bass_kernel_spmd(prog, in_maps, core_ids=list(range(8)))
    return [np.asarray(r["yT"]) for r in res.results]


def kernel(**inputs):
    inp = {k: np.asarray(v) for k, v in inputs.items()}
    x = inp["x"].astype(np.float32, copy=False)
    wts = {n: np.ascontiguousarray(inp[n], dtype=np.float32) for n in _WNAMES}
    xTs, pvs = [], []
    for i in range(8):
        b, half = i // 2, i % 2
        t0 = _tok0(half)
        xt = x[b, t0:t0 + TT, :].T.reshape(NDC, 128, TT).transpose(1, 0, 2)
        xTs.append(np.ascontiguousarray(xt))
        pvs.append(_pack_pv(inp, half))
    ys = _run(_get_prog((0, 1), True), xTs, pvs, wts)
    out = np.empty((4, SEQ, D), np.float32)
    for i in range(8):
        b, half = i // 2, i % 2
        y = ys[i].transpose(1, 0, 2).reshape(D, TT).T
        if half == 0:
            out[b, 0:TT, :] = y
        else:
            out[b, TT:SEQ, :] = y[HALO:, :]
    return out
```
